# Optimizing a Trainium2 kernel written in Bass

```python
import math
import jax, jax.numpy as jnp
from jax import lax
import numpy as np

D_MODEL = 1024
BATCH = 8
SEQ = 8192
DEPTH = 1

BRANCH_W = D_MODEL
N_BRANCH = 3
LRU_W = BRANCH_W
LRU_BLOCKS = 16
LRU_BLOCK_W = LRU_W // LRU_BLOCKS
LRU_C = 8.0
CONV_W = 4
FOX_HEAD_DIM = 64
FOX_HEADS = BRANCH_W // FOX_HEAD_DIM
FOX_W = FOX_HEADS * FOX_HEAD_DIM
Q_BLOCK = 128
MEM_LEN = 256
MEM_HEADS = 4
MEM_HEAD_DIM = BRANCH_W // MEM_HEADS
MEM_W = MEM_HEADS * MEM_HEAD_DIM
RMS_EPS = 1e-6
NEG_INF = -1e30
SPLIT_SIZES = (LRU_W, LRU_W, FOX_W, FOX_W, FOX_W, FOX_HEADS, FOX_W, MEM_W, MEM_W, N_BRANCH * D_MODEL)
IN_COLS = sum(SPLIT_SIZES)

kernel_name = "hybrid_rglru_fox_memxattn_gated_merge"


def rmsnorm(x, g):
    xf = x.astype(jnp.float32)
    y = xf * lax.rsqrt(jnp.mean(xf * xf, axis=-1, keepdims=True) + RMS_EPS)
    return (y * g.astype(jnp.float32)).astype(x.dtype)


def causal_depthwise_conv(u, w, b):
    C = u.shape[-1]
    out = lax.conv_general_dilated(
        u, w.reshape(CONV_W, 1, C).astype(u.dtype),
        window_strides=(1,), padding=[(CONV_W - 1, 0)],
        dimension_numbers=("NWC", "WIO", "NWC"), feature_group_count=C)
    return out + b.astype(u.dtype)


def rg_lru(xc, w_r, b_r, w_i, b_i, lam):
    B, S, W = xc.shape
    xg = xc.reshape(B, S, LRU_BLOCKS, LRU_BLOCK_W)
    r = jax.nn.sigmoid(jnp.einsum("bsgi,gij->bsgj", xg, w_r) + b_r).reshape(B, S, W)
    i = jax.nn.sigmoid(jnp.einsum("bsgi,gij->bsgj", xg, w_i) + b_i).reshape(B, S, W)
    log_a = (-LRU_C * r.astype(jnp.float32)) * jax.nn.softplus(-lam.astype(jnp.float32))
    a = jnp.exp(log_a)
    mult = jnp.sqrt(-jnp.expm1(2.0 * log_a))
    u = mult * (i * xc).astype(jnp.float32)

    def combine(left, right):
        a1, b1 = left
        a2, b2 = right
        return a1 * a2, a2 * b1 + b2

    _, h = lax.associative_scan(combine, (a, u), axis=1)
    return h.astype(xc.dtype)


def forgetting_attention(q, k, v, log_f):
    B, S, H, Dh = q.shape
    nb = S // Q_BLOCK
    scale = 1.0 / math.sqrt(Dh)
    c = jnp.cumsum(log_f, axis=1)
    c_k = jnp.transpose(c, (0, 2, 1))
    kf = k.astype(jnp.float32)
    vf = v.astype(jnp.float32)
    qb = jnp.transpose(q.reshape(B, nb, Q_BLOCK, H, Dh), (1, 0, 2, 3, 4))
    cb = jnp.transpose(c.reshape(B, nb, Q_BLOCK, H), (1, 0, 3, 2))
    kpos = jnp.arange(S)

    def one_block(args):
        blk, qi, ci = args
        s = jnp.einsum("bqhd,bkhd->bhqk", qi.astype(jnp.float32), kf) * scale
        s = s + (ci[..., :, None] - c_k[..., None, :])
        qpos = blk * Q_BLOCK + jnp.arange(Q_BLOCK)
        mask = kpos[None, :] <= qpos[:, None]
        s = jnp.where(mask, s, NEG_INF)
        p = jax.nn.softmax(s, axis=-1)
        return jnp.einsum("bhqk,bkhd->bqhd", p, vf)

    o = lax.map(one_block, (jnp.arange(nb), qb, cb))
    return jnp.transpose(o, (1, 0, 2, 3, 4)).reshape(B, S, H, Dh).astype(q.dtype)


def memory_attention(qm, mk, mv):
    scale = 1.0 / math.sqrt(qm.shape[-1])
    s = jnp.einsum("bshd,bmhd->bhsm", qm.astype(jnp.float32), mk.astype(jnp.float32)) * scale
    p = jax.nn.softmax(s, axis=-1)
    o = jnp.einsum("bhsm,bmhd->bshd", p, mv.astype(jnp.float32))
    return o.astype(qm.dtype)


def setup_inputs(seed: int = 0) -> dict:
    key = jax.random.key(seed)
    ks = jax.random.split(key, 20)
    f32 = jnp.float32
    x = jax.random.normal(ks[0], (BATCH, SEQ, D_MODEL), f32)
    mem = jax.random.normal(ks[1], (BATCH, MEM_LEN, D_MODEL), f32)
    g_pre = 1.0 + 0.1 * jax.random.normal(ks[2], (D_MODEL,), f32)
    w_in = jax.random.normal(ks[3], (D_MODEL, IN_COLS), f32) * D_MODEL ** -0.5
    conv_w = jax.random.normal(ks[4], (CONV_W, LRU_W), f32) * CONV_W ** -0.5
    conv_b = 0.02 * jax.random.normal(ks[5], (LRU_W,), f32)
    w_lru_r = jax.random.normal(ks[6], (LRU_BLOCKS, LRU_BLOCK_W, LRU_BLOCK_W), f32) * LRU_BLOCK_W ** -0.5
    b_lru_r = 0.02 * jax.random.normal(ks[7], (LRU_BLOCKS, LRU_BLOCK_W), f32)
    w_lru_i = jax.random.normal(ks[8], (LRU_BLOCKS, LRU_BLOCK_W, LRU_BLOCK_W), f32) * LRU_BLOCK_W ** -0.5
    b_lru_i = 0.02 * jax.random.normal(ks[9], (LRU_BLOCKS, LRU_BLOCK_W), f32)
    a0 = jax.random.uniform(ks[10], (LRU_W,), f32, minval=0.9, maxval=0.999)
    lru_lambda = jnp.log(a0) - jnp.log1p(-a0)
    b_forget = 3.0 + 0.5 * jax.random.normal(ks[11], (FOX_HEADS,), f32)
    g_mem = 1.0 + 0.1 * jax.random.normal(ks[12], (D_MODEL,), f32)
    w_mem_k = jax.random.normal(ks[13], (D_MODEL, MEM_W), f32) * D_MODEL ** -0.5
    w_mem_v = jax.random.normal(ks[14], (D_MODEL, MEM_W), f32) * D_MODEL ** -0.5
    w_branch = jax.random.normal(ks[15], (N_BRANCH, BRANCH_W, D_MODEL), f32) * BRANCH_W ** -0.5
    b_merge = 0.02 * jax.random.normal(ks[16], (N_BRANCH, D_MODEL), f32)
    w_out = jax.random.normal(ks[17], (D_MODEL, D_MODEL), f32) * D_MODEL ** -0.5
    g_post = 1.0 + 0.1 * jax.random.normal(ks[18], (D_MODEL,), f32)
    return {"x": x, "mem": mem, "g_pre": g_pre, "w_in": w_in, "conv_w": conv_w, "conv_b": conv_b,
            "w_lru_r": w_lru_r, "b_lru_r": b_lru_r, "w_lru_i": w_lru_i, "b_lru_i": b_lru_i,
            "lru_lambda": lru_lambda, "b_forget": b_forget, "g_mem": g_mem, "w_mem_k": w_mem_k,
            "w_mem_v": w_mem_v, "w_branch": w_branch, "b_merge": b_merge, "w_out": w_out,
            "g_post": g_post}


def hybrid_layer(x, mem, g_pre, w_in, conv_w, conv_b, w_lru_r, b_lru_r, w_lru_i, b_lru_i,
                 lru_lambda, b_forget, g_mem, w_mem_k, w_mem_v, w_branch, b_merge, w_out, g_post):
    B, S, D = x.shape
    Bm, M, _ = mem.shape
    xn = rmsnorm(x, g_pre)
    z = xn @ w_in
    offsets = [int(o) for o in np.cumsum(SPLIT_SIZES)[:-1]]
    (a_x, a_gate, f_q, f_k, f_v, f_logit, f_gate, m_q, m_gate, merge_logit) = jnp.split(z, offsets, axis=-1)

    xc = causal_depthwise_conv(a_x, conv_w, conv_b)
    h = rg_lru(xc, w_lru_r, b_lru_r, w_lru_i, b_lru_i, lru_lambda)
    y_a = h * jax.nn.silu(a_gate)

    q = f_q.reshape(B, S, FOX_HEADS, FOX_HEAD_DIM)
    k = f_k.reshape(B, S, FOX_HEADS, FOX_HEAD_DIM)
    v = f_v.reshape(B, S, FOX_HEADS, FOX_HEAD_DIM)
    log_f = jax.nn.log_sigmoid((f_logit + b_forget).astype(jnp.float32))
    o_b = forgetting_attention(q, k, v, log_f)
    y_b = o_b.reshape(B, S, FOX_W) * jax.nn.silu(f_gate)

    mn = rmsnorm(mem, g_mem)
    mk = (mn @ w_mem_k).reshape(Bm, M, MEM_HEADS, MEM_HEAD_DIM)
    mv = (mn @ w_mem_v).reshape(Bm, M, MEM_HEADS, MEM_HEAD_DIM)
    o_m = memory_attention(m_q.reshape(B, S, MEM_HEADS, MEM_HEAD_DIM), mk, mv)
    y_m = o_m.reshape(B, S, MEM_W) * jax.nn.silu(m_gate)

    ys = jnp.stack([y_a, y_b, y_m], axis=2)
    proj = jnp.einsum("bsnw,nwd->bsnd", ys, w_branch)
    gates = jax.nn.sigmoid(merge_logit.reshape(B, S, N_BRANCH, D) + b_merge)
    merged = jnp.sum(gates * proj, axis=2)
    out = merged @ w_out
    return x + rmsnorm(out, g_post)


def reference(x, mem, g_pre, w_in, conv_w, conv_b, w_lru_r, b_lru_r, w_lru_i, b_lru_i,
              lru_lambda, b_forget, g_mem, w_mem_k, w_mem_v, w_branch, b_merge, w_out, g_post):
    h = x
    for _ in range(DEPTH):
        h = hybrid_layer(h, mem, g_pre, w_in, conv_w, conv_b, w_lru_r, b_lru_r, w_lru_i, b_lru_i,
                         lru_lambda, b_forget, g_mem, w_mem_k, w_mem_v, w_branch, b_merge, w_out, g_post)
    return h
```

```python
import numpy as np
from contextlib import ExitStack
import concourse.bass as bass
import concourse.mybir as mybir
from concourse.bass_utils import run_bass_kernel_spmd

F32 = mybir.dt.float32
BF16 = mybir.dt.bfloat16
AF = mybir.ActivationFunctionType
ALU = mybir.AluOpType

S = 8192
D = 1024
TC = 512
NCH_FULL = S // TC
IN_COLS = 11280
C_AX, C_AG, C_Q, C_K, C_V, C_FL, C_FG, C_MQ, C_MG, C_MERGE = 0, 1024, 2048, 3072, 4096, 5120, 5136, 6160, 7184, 8208
SEG = 1536
NEG = -30000.0
EPS = 1e-6
STRICT = True
NBG_PRINT = False


class Rec:
    def __init__(self):
        self.q = {e: [] for e in ("pe", "act", "dve", "pool", "sp")}
        self.cnt = {}
        self.known = {e: {} for e in self.q}
        self.lw = {}
        self.rd = {}
        self.rr = {}

    def _deps(self, eng, reads, writes):
        ev = {}

        def need(e, kind):
            if e is None:
                return
            k, v = e
            if k == eng and (eng == "pe" or (kind != "raw" and not STRICT)):
                return
            if ev.get(k, 0) < v:
                ev[k] = v

        for r in reads:
            need(self.lw.get(r), "raw")
            if isinstance(r, tuple) and r[0] == "ps":
                for k, v in self.rd.get(r, {}).items():
                    if k != eng:
                        need((k, v), "rar")
        for w in writes:
            need(self.lw.get(w), "waw")
            for k, v in self.rd.get(w, {}).items():
                need((k, v), "war")
        waits = []
        kn = self.known[eng]
        for k, v in ev.items():
            if kn.get(k, 0) >= v:
                continue
            kn[k] = v
            waits.append((k, v))
        return waits

    def _commit(self, reads, writes, event):
        k, v = event
        for r in reads:
            d = self.rd.setdefault(r, {})
            if d.get(k, 0) < v:
                d[k] = v
        for w in writes:
            self.lw[w] = event
            self.rd[w] = {}

    def op(self, eng, fn, reads=(), writes=()):
        waits = self._deps(eng, reads, writes)
        self.cnt[eng] = self.cnt.get(eng, 0) + 1
        self.q[eng].append((waits, fn, (eng, 1)))
        self._commit(reads, writes, (eng, self.cnt[eng]))

    def dma(self, queue, out, in_, reads=(), writes=(), nsem=8, **kw):
        i = self.rr.get(queue, 0)
        self.rr[queue] = (i + 1) % nsem
        key = "d_%s_%d" % (queue, i)
        waits = self._deps(queue, reads, writes)
        prev = self.cnt.get(key, 0)
        if prev and self.known[queue].get(key, 0) < prev:
            self.known[queue][key] = prev
            waits.append((key, prev))
        self.cnt[key] = prev + 16
        self.q[queue].append((waits, (lambda e, o=out, i_=in_, kw=kw: e.dma_start(out=o, in_=i_, **kw)), (key, 16)))
        self._commit(reads, writes, (key, prev + 16))

    def alias(self, new, old):
        d = dict(self.rd.get(new, {}))
        e0 = self.lw.get(new)
        if e0 is not None and d.get(e0[0], 0) < e0[1]:
            d[e0[0]] = e0[1]
        for o in old:
            e = self.lw.get(o)
            if e is not None and d.get(e[0], 0) < e[1]:
                d[e[0]] = e[1]
            for k, v in self.rd.get(o, {}).items():
                if d.get(k, 0) < v:
                    d[k] = v
        self.rd[new] = d

    def emit(self, nc, block, stack):
        keys = sorted(self.cnt.keys())
        sems = {k: stack.enter_context(nc.semaphore("s_" + k)) for k in keys}
        final = [(k, self.cnt[k]) for k in keys]
        decos = {"pe": block.tensor, "act": block.scalar, "dve": block.vector, "pool": block.gpsimd, "sp": block.sync}
        for name in ("sp", "pool", "pe", "act", "dve"):
            ops = self.q[name]

            def body(e, ops=ops, name=name):
                for waits, fn, inc in ops:
                    for k, v in waits:
                        e.wait_ge(sems[k], v)
                    fn(e).then_inc(sems[inc[0]], inc[1])
                if name == "sp":
                    for k, v in final:
                        e.wait_ge(sems[k], v)

            decos[name](body)


def build(nch, dbg=()):
    nc = bass.Bass("TRN2", target_bir_lowering=False)
    R = Rec()
    dbg_out = {}

    def dump(name, ap, shape, reads):
        if name not in dbg:
            return
        t = nc.dram_tensor("dbg_" + name, list(shape), F32, kind="ExternalOutput")
        dbg_out[name] = t
        R.dma("pool", t.ap(), ap, reads=reads, writes=[("dbg", name)])

    def din(name, shape, dt=F32):
        return nc.dram_tensor(name, list(shape), dt, kind="ExternalInput")

    x_t = din("x", [S, D])
    mem_t = din("mem", [256, D])
    w_in_t = din("w_in", [D, IN_COLS])
    w_mk_t = din("w_mem_k", [D, D])
    w_mv_t = din("w_mem_v", [D, D])
    w_br_t = din("w_branch", [3, D, D])
    w_out_t = din("w_out", [D, D])
    small_t = din("small", [128, 80])
    bfg_t = din("b_forget", [16, 1])
    gpost_t = din("g_post", [128, D])
    lruw_t = din("lruw", [2, 128, 8, 128])
    ident_t = din("ident", [128, 128])
    mask_t = din("maskT", [128, 128])
    ones_t = din("ones", [128, 512])
    y_t = nc.dram_tensor("y", [S, D], F32, kind="ExternalOutput")

    tiles = {}
    tlist = []

    def deftile(name, src, row0, col0, kind):
        tiles[name] = len(tlist)
        tlist.append((name, src, row0, col0, kind))

    for j in range(2):
        deftile("q%d" % j, w_in_t, 0, C_Q + 512 * j, "k128")
        deftile("k%d" % j, w_in_t, 0, C_K + 512 * j, "k128")
        deftile("v%d" % j, w_in_t, 0, C_V + 512 * j, "k128")
        deftile("fg%d" % j, w_in_t, 0, C_FG + 512 * j, "k128")
        deftile("ax%d" % j, w_in_t, 0, C_AX + 512 * j, "k128")
        deftile("ag%d" % j, w_in_t, 0, C_AG + 512 * j, "k128")
        deftile("mq%d" % j, w_in_t, 0, C_MQ + 512 * j, "k128")
        deftile("mg%d" % j, w_in_t, 0, C_MG + 512 * j, "k128")
        for n in range(3):
            deftile("mr%d_%d" % (n, j), w_in_t, 0, C_MERGE + n * 1024 + 512 * j, "k128")
        deftile("b0_%d" % j, w_br_t, 0, 512 * j, "k128")
        deftile("b2_%d" % j, w_br_t, 2 * D, 512 * j, "k128")
        deftile("wo%d" % j, w_out_t, 0, 512 * j, "k128")
        deftile("wmk%d" % j, w_mk_t, 0, 512 * j, "k128")
        deftile("wmv%d" % j, w_mv_t, 0, 512 * j, "k128")
    for j in range(2):
        deftile("b1_%d" % j, w_br_t, D, 512 * j, "k128")
    NT = len(tlist)
    wscr = nc.dram_tensor("wscr", [NT, 128, 4096], BF16)
    kscr = nc.dram_tensor("kscr", [16, 70, S], BF16)
    vscr = nc.dram_tensor("vscr", [16, 128, 64, 66], BF16)

    def src2d(t):
        a = t.ap()
        if len(t.shape) == 3:
            a = a.rearrange("n r c -> (n r) c")
        return a

    with ExitStack() as st:
        def sb(name, shape, dt):
            return st.enter_context(nc.sbuf_tensor("sb_" + name, list(shape), dt))

        ps = st.enter_context(nc.psum_tensor("ps", [128, 4096], F32))

        def bank(b, n=1):
            return ps[:, b * 512:(b + n) * 512]

        NW = 4
        wbuf = [sb("wbuf%d" % i, [128, 4096], BF16) for i in range(NW)]
        Treg = sb("Treg", [128, 7168], F32)
        xtm = Treg[:, 0:4096].rearrange("p (t d) -> p t d", t=4)
        junk = sb("junk", [128, 1024], F32)
        xnT = sb("xnT", [128, 8, 512], BF16)
        Qaug = sb("Qaug", [70, 16, 512], BF16)
        NKB = 3
        kbuf = [sb("kbuf%d" % i, [70, SEG], BF16) for i in range(NKB)]
        vbuf = [sb("vbuf%d" % i, [128, SEG // 128, 66], BF16) for i in range(NKB)]
        sgT = sb("sgT", [64, 16, 512], BF16)
        ybT = sgT
        vst = sb("vst", [128, 16, 4, 66], BF16)
        NPT = 4
        PT = [sb("PT%d" % i, [128, 512], BF16) for i in range(NPT)]
        merged = sb("merged", [128, 8, 512], F32)
        yT = sb("yT", [128, 8, 512], BF16)
        mergedT = yT
        kst = yT
        mqT = sb("mqT", [128, 8, 512], BF16)
        mkT = sb("mkT", [128, 8, 256], BF16)
        mv = sb("mv", [128, 2, 1024], BF16)
        mnT = yT[:, :, 0:256]
        gpost = sb("gpost", [128, 1024], F32)
        small = sb("small", [128, 80], F32)
        spc = sb("spc", [128, 16], F32)
        lruW = sb("lruW", [128, 2, 8, 128], BF16)
        wfl = sb("wfl", [128, 8, 16], BF16)
        identf = sb("identf", [128, 128], F32)
        identb = sb("identb", [128, 128], BF16)
        maskf = junk[:, 0:128]
        maskb = sb("maskb", [128, 128], BF16)
        onesf = sb("onesf", [128, 512], BF16)
        onesb = onesf[:, 0:128]
        epsc = sb("epsc", [128, 1], F32)
        eps4 = sb("eps4", [128, 1], F32)
        onec = sb("onec", [128, 64], F32)
        nbf = sb("nbf", [16, 1], F32)
        bfg = sb("bfg", [16, 1], F32)
        ssq = sb("ssq", [128, 8], F32)
        stdt = sb("stdt", [128, 8], F32)
        rstd = sb("rstd", [128, 8], F32)
        ssq2 = sb("ssq2", [128, 4], F32)
        std2 = sb("std2", [128, 4], F32)
        rstd2 = sb("rstd2", [128, 4], F32)
        axe = [sb("axe%d" % i, [128, 515], F32) for i in range(2)]
        def Tt(k):
            return Treg[:, k * 512:(k + 1) * 512]
        xc = [Tt(0), Tt(1)]
        lr = [Tt(2), Tt(3)]
        li = [Tt(4), Tt(5)]
        la = [Tt(6), Tt(7)]
        lm = [Tt(8), Tt(9)]
        hh = [Tt(10), Tt(11)]
        sga = [Tt(12), Tt(13)]
        xcbt = sb("xcbt", [128, 2, 512], BF16)
        Pm1 = sb("Pm1", [128, 2, 512], BF16)
        xcb = [xcbt[:, 0, :], xcbt[:, 1, :]]
        T_LOW = [("xc", 0), ("xc", 1), ("lr", 0), ("lr", 1), ("li", 0), ("li", 1), ("la", 0), ("la", 1)]
        axcar = sb("axcar", [128, 8, 3], F32)
        hlast = [sb("hlast%d" % i, [128, 8], F32) for i in range(2)]
        gt = [sb("gt%d" % i, [128, 512], F32) for i in range(2)]
        tmpm = [sb("tmpm%d" % i, [128, 512], F32) for i in range(2)]
        fe = junk[0:16, 0:512]
        fsp = junk[0:16, 512:1024]
        cneg = [sb("cneg%d" % i, [16, 512], F32) for i in range(2)]
        kp = sb("kp", [16, 3, 512], BF16)
        qp = sb("qp", [16, 3, 512], BF16)
        recm = tmpm
        recrow = [junk[:, 0:512], junk[:, 512:1024]]
        tmpn = [junk[0:64, 0:512], junk[0:64, 512:1024]]
        cres = recm[0][0:16, :]
        ot = [Treg[:, 4096:5120], Treg[:, 5120:6144]]
        xr = [Treg[:, 6144:7168], Treg[:, 6144:7168]]

        O_GPRE, O_CW, O_CB, O_BR, O_BI, O_LAM, O_GMEM, O_BM = 0, 8, 40, 48, 56, 64, 72, 80
        bm_t = din("bmerge", [128, 24])
        bmerge = sb("bmerge", [128, 24], F32)

        state = {"b": 0}

        state["bg"] = False
        state["bb"] = 0

        def nb(n=1):
            if state["bg"]:
                if n == 2:
                    state["bb"] = 0
                    return 6
                b = 6 + state["bb"]
                state["bb"] ^= 1
                return b
            b = state["b"]
            if n == 2 and b % 2:
                b = (b + 1) % 8
            state["b"] = (b + n) % 8
            return b

        def pres(b, n=1):
            return [("ps", b + i) for i in range(n)]

        wseq = []
        wstate = {"issued": 0, "pos": 0}

        def wissue_upto(n):
            while wstate["issued"] < min(n, len(wseq)):
                j = wstate["issued"]
                tid = tiles[wseq[j]]
                slot = j % NW
                np_ = 128 if tlist[tid][4] == "k128" else 64
                R.dma("sp", wbuf[slot][0:np_, :], wscr.ap()[tid][0:np_, :], reads=[("wscr", tid)], writes=[("w", slot)])
                wstate["issued"] += 1

        def wget(name):
            j = wstate["pos"]
            assert wseq[j] == name, (wseq[j], name, j)
            wissue_upto(j + 2)
            wstate["pos"] += 1
            slot = j % NW
            kind = tlist[tiles[name]][4]
            if kind == "k128":
                view = wbuf[slot][:, :].rearrange("p (k c) -> p k c", k=8)
            else:
                view = wbuf[slot][0:64, :].rearrange("p (h c) -> p h c", h=16)
            return view, ("w", slot)

        setup_seq = ["wmk0", "wmk1", "wmv0", "wmv1"]
        chunk_seq = (["q0", "q1", "k0", "k1", "v0", "v1", "fg0", "fg1"]
                     + ["mq0", "mq1", "mg0", "mg1", "b2_0", "mr2_0", "b2_1", "mr2_1"]
                     + ["ax0", "ag0", "ax1", "ag1", "b0_0", "mr0_0", "b0_1", "mr0_1"]
                     + ["b1_0", "mr1_0", "b1_1", "mr1_1", "wo0", "wo1"])
        wseq.extend(setup_seq)
        for _ in range(nch):
            wseq.extend(chunk_seq)

        order = [tiles[n] for n in setup_seq] + [tiles[n] for n in chunk_seq]
        for tid in order:
            name, src, row0, col0, kind = tlist[tid]
            a = src2d(src)
            if kind == "k128":
                s_ap = a[row0:row0 + 1024, col0:col0 + 512].rearrange("(k p) c -> p k c", p=128)
                d_ap = wscr.ap()[tid].rearrange("p (k c) -> p k c", k=8)
            else:
                s_ap = a[row0:row0 + 1024, col0:col0 + 256].rearrange("(h p) c -> p h c", p=64)
                d_ap = wscr.ap()[tid][0:64, :].rearrange("p (h c) -> p h c", h=16)
            R.dma("pool", d_ap, s_ap, writes=[("wscr", tid)])
        for h in range(16):
            R.dma("pool", kscr.ap()[h, 64:67, :].rearrange("r (a t) -> (r a) t", t=512), ones_t.ap()[0:48, :],
                  writes=[("kones", h)])
        R.op("pool", lambda e: e.memset(epsc[:, :], EPS), writes=["epsc"])
        R.op("pool", lambda e: e.memset(onec[:, :], 1.0), writes=["onec"])
        R.dma("sp", small[:, :], small_t.ap(), writes=["small"])
        R.dma("sp", bmerge[:, :], bm_t.ap(), writes=["bmerge"])
        R.dma("sp", bfg[:, :], bfg_t.ap(), writes=["bfg"])
        R.dma("sp", identf[:, :], ident_t.ap(), writes=["identf"])
        R.dma("sp", maskf, mask_t.ap(), writes=["junk"])
        R.dma("pool", onesf[:, :], ones_t.ap(), writes=["onesf", "onesb"])
        R.dma("sp", gpost[:, :], gpost_t.ap(), writes=["gpost"])
        R.dma("pool", lruW[:, :, :, :], lruw_t.ap().rearrange("g p f o -> p g f o"), writes=["lruW"])
        R.dma("pool", wfl[:, :, :], w_in_t.ap()[:, C_FL:C_FL + 16].rearrange("(k p) c -> p k c", p=128), writes=["wfl"])
        R.op("dve", lambda e: e.tensor_copy(out=identb[:, :], in_=identf[:, :]), reads=["identf"], writes=["identb"])
        R.op("dve", lambda e: e.tensor_copy(out=maskb[:, :], in_=maskf), reads=["junk"], writes=["maskb"])
        R.op("dve", lambda e: e.tensor_scalar(nbf[:, :], bfg[:, :], -1.0, None, ALU.mult), reads=["bfg"], writes=["nbf"])
        R.op("act", lambda e: e.activation(out=spc[:, 0:8], in_=small[:, 64:72], func=AF.Exp, scale=-1.0),
             reads=["small"], writes=["spc"])
        R.op("act", lambda e: e.activation(out=spc[:, 0:8], in_=spc[:, 0:8], func=AF.Ln, bias=onec[:, 0:1]),
             reads=["spc", "onec"], writes=["spc"])
        R.op("dve", lambda e: e.tensor_scalar(spc[:, 8:16], spc[:, 0:8], -8.0, None, ALU.mult), reads=["spc"], writes=["spc2"])
        R.op("dve", lambda e: e.tensor_scalar(spc[:, 0:8], spc[:, 0:8], -4.0, None, ALU.mult), reads=["spc", "spc2"], writes=["spc"])
        R.op("dve", lambda e: e.tensor_scalar(small[:, 48:64], small[:, 48:64], 0.5, None, ALU.mult), reads=["small"], writes=["small"])
        R.op("dve", lambda e: e.tensor_scalar(bmerge[:, :], bmerge[:, :], 0.5, None, ALU.mult), reads=["bmerge"], writes=["bmerge"])
        R.op("pool", lambda e: e.memset(eps4[:, :], 4.0 * EPS), writes=["eps4"])
        R.op("pool", lambda e: e.memset(vst[:, :, :, 64:65], 1.0), writes=["vst1"])
        R.op("pool", lambda e: e.memset(vst[:, :, :, 65:66], 0.0), writes=["vst0"])
        R.op("pool", lambda e: e.memset(Qaug[64:70, :, :], 1.0), writes=["Qones"])
        R.op("pool", lambda e: e.memset(axcar[:, :, :], 0.0), writes=["axcar"])
        R.op("pool", lambda e: e.memset(hlast[1][:, :], 0.0), writes=[("hlast", 1)])
        R.op("pool", lambda e: e.memset(cneg[1][:, :], 0.0), writes=[("cneg", 1)])

        def rms_tm(src_ap_fn, ntb, base):
            for tb in range(ntb):
                R.op("act", lambda e, tb=tb: e.activation(out=junk[:, :], in_=src_ap_fn(tb), func=AF.Square,
                                                          accum_out=ssq[:, base + tb:base + tb + 1]),
                     reads=["xtm"], writes=["junk", ("ssq", base + tb)])
            R.op("act", lambda e: e.activation(out=stdt[:, base:base + ntb], in_=ssq[:, base:base + ntb], func=AF.Sqrt,
                                               scale=1.0 / D, bias=epsc[:, 0:1]),
                 reads=[("ssq", base + t) for t in range(ntb)] + ["epsc"], writes=[("std", base)])
            R.op("dve", lambda e: e.reciprocal(out=rstd[:, base:base + ntb], in_=stdt[:, base:base + ntb]),
                 reads=[("std", base)], writes=[("rstd", base)])

        R.dma("sp", xtm[:, 0:2, :], mem_t.ap().rearrange("(t p) d -> p t d", p=128), writes=["xtm"])
        rms_tm(lambda tb: xtm[:, tb, :], 2, 4)
        for tb in range(2):
            R.op("dve", lambda e, tb=tb: e.tensor_scalar(xtm[:, tb, :], xtm[:, tb, :], rstd[:, 4 + tb:5 + tb], None, ALU.mult),
                 reads=["xtm", ("rstd", 4)], writes=["xtm"])
        for kc in range(8):
            b = nb()
            for tb in range(2):
                R.op("pe", lambda e, kc=kc, tb=tb, b=b: e.transpose(out=bank(b)[:, tb * 128:(tb + 1) * 128],
                                                                    in_=xtm[:, tb, kc * 128:(kc + 1) * 128], identity=identf[:, :]),
                     reads=["xtm", "identf"], writes=pres(b))
            R.op("dve", lambda e, kc=kc, b=b: e.tensor_scalar(mnT[:, kc, :], bank(b)[:, 0:256], small[:, 72 + kc:73 + kc], None, ALU.mult),
                 reads=pres(b) + ["small"], writes=["yT"])
        for j in range(2):
            wv_, wr_ = wget("wmk%d" % j)
            for f4 in range(4):
                fc = j * 4 + f4
                b = nb()
                for kc in range(8):
                    R.op("pe", lambda e, kc=kc, b=b, f4=f4, wv_=wv_: e.matmul(bank(b)[:, 0:256], lhsT=wv_[:, kc, f4 * 128:(f4 + 1) * 128],
                                                                            rhs=mnT[:, kc, :], start=(kc == 0), stop=(kc == 7)),
                         reads=[wr_, "yT"], writes=pres(b))
                R.op("dve", lambda e, fc=fc, b=b: e.tensor_copy(out=mkT[:, fc, :], in_=bank(b)[:, 0:256]), reads=pres(b), writes=["mkT"])
        for j in range(2):
            wv_, wr_ = wget("wmv%d" % j)
            for mb in range(2):
                b = nb()
                for kc in range(8):
                    R.op("pe", lambda e, kc=kc, b=b, mb=mb, wv_=wv_: e.matmul(bank(b), lhsT=mnT[:, kc, mb * 128:(mb + 1) * 128],
                                                                            rhs=wv_[:, kc, :], start=(kc == 0), stop=(kc == 7)),
                         reads=[wr_, "yT"], writes=pres(b))
                R.op("dve", lambda e, j=j, mb=mb, b=b: e.tensor_copy(out=mv[:, mb, j * 512:(j + 1) * 512], in_=bank(b)), reads=pres(b), writes=["mv"])

        def proj_fm(b, wv_, wr_, col0, M, rhs_fn, rhs_res, nk=8, kpart=128):
            for kc in range(nk):
                R.op("pe", lambda e, kc=kc: e.matmul(bank(b)[0:M, :], lhsT=wv_[0:kpart, kc, col0:col0 + M], rhs=rhs_fn(kc),
                                                     start=(kc == 0), stop=(kc == nk - 1)),
                     reads=[wr_] + list(rhs_res), writes=pres(b))

        def proj_fm_g(b, wv_, wr_, col0, M, rhs_fn, rhs_res, nk=8, kpart=128, every=2):
            for kc in range(nk):
                R.op("pe", lambda e, kc=kc: e.matmul(bank(b)[0:M, :], lhsT=wv_[0:kpart, kc, col0:col0 + M], rhs=rhs_fn(kc),
                                                     start=(kc == 0), stop=(kc == nk - 1)),
                     reads=[wr_] + list(rhs_res), writes=pres(b))
                if kc % every == every - 1 and kc != nk - 1:
                    yield

        def xn_rhs(kc):
            return xnT[:, kc, :]

        def rr2(g0, g1):
            gs = [g0, g1]
            while gs:
                for g in list(gs):
                    try:
                        next(g)
                        yield
                    except StopIteration:
                        gs.remove(g)

        def merge_oc(n, oc, o4, bk, wbv, wbr, wmv_, wmr, y_fn, y_res, nk, first, last):
            sid = oc % 2
            g = gt[sid]
            yield from proj_fm_g(bk, wmv_, wmr, o4 * 128, 128, xn_rhs, ["xnT"])
            yield
            R.op("act", lambda e: e.activation(out=g[:, :], in_=bank(bk), func=AF.Tanh, scale=0.5, bias=bmerge[:, n * 8 + oc:n * 8 + oc + 1]),
                 reads=pres(bk) + ["bmerge"], writes=[("gt", sid)])
            yield
            yield from proj_fm_g(bk, wbv, wbr, o4 * 128, 128, y_fn, y_res, nk=nk)
            yield
            if first:
                R.op("dve", lambda e: e.scalar_tensor_tensor(out=merged[:, oc, :], in0=g[:, :], scalar=1.0, in1=bank(bk), op0=ALU.add, op1=ALU.mult),
                     reads=pres(bk) + [("gt", sid)], writes=[("merged", oc)])
            else:
                t = tmpm[sid]
                R.op("dve", lambda e: e.scalar_tensor_tensor(out=t[:, :], in0=g[:, :], scalar=1.0, in1=bank(bk), op0=ALU.add, op1=ALU.mult),
                     reads=pres(bk) + [("gt", sid)], writes=[("tmpm", sid)])
                yield
                if last:
                    R.op("dve", lambda e: e.tensor_tensor(out=mergedT[:, oc, :], in0=merged[:, oc, :], in1=t[:, :], op=ALU.add),
                         reads=[("tmpm", sid), ("merged", oc)], writes=["yT"])
                else:
                    R.op("dve", lambda e: e.tensor_tensor(out=merged[:, oc, :], in0=merged[:, oc, :], in1=t[:, :], op=ALU.add),
                         reads=[("tmpm", sid), ("merged", oc)], writes=[("merged", oc)])
            yield

        def branch_merge(n, c, wb_names, wm_names, y_fn, y_res, nk, kpart, first, last):
            assert kpart == 128
            for half in range(2):
                wbv, wbr = wget(wb_names[half])
                wmv_, wmr = wget(wm_names[half])

                def stream(sid, half=half, wbv=wbv, wbr=wbr, wmv_=wmv_, wmr=wmr):
                    for o4 in (sid, sid + 2):
                        yield from merge_oc(n, half * 4 + o4, o4, 6 + sid, wbv, wbr, wmv_, wmr, y_fn, y_res, nk, first, last)

                yield from rr2(stream(0), stream(1))

        kvstate = {"n": 0}
        for c in range(nch):
            t0 = c * TC
            if c == 0:
                R.alias("xtm", T_LOW)
                R.dma("sp", xtm[:, :, :], x_t.ap()[t0:t0 + TC, :].rearrange("(t p) d -> p t d", p=128), writes=["xtm"])
                rms_tm(lambda tb: xtm[:, tb, :], 4, 0)
                for tb in range(4):
                    R.op("dve", lambda e, tb=tb: e.tensor_scalar(xtm[:, tb, :], xtm[:, tb, :], rstd[:, tb:tb + 1], None, ALU.mult),
                         reads=["xtm", ("rstd", 0)], writes=["xtm"])
            for kc in range(8):
                b = nb()
                for tb in range(4):
                    R.op("pe", lambda e, kc=kc, tb=tb, b=b: e.transpose(out=bank(b)[:, tb * 128:(tb + 1) * 128],
                                                                        in_=xtm[:, tb, kc * 128:(kc + 1) * 128], identity=identf[:, :]),
                         reads=["xtm", "identf"], writes=pres(b))
                R.op("dve", lambda e, kc=kc, b=b: e.tensor_scalar(xnT[:, kc, :], bank(b), small[:, kc:kc + 1], None, ALU.mult),
                     reads=pres(b) + ["small"], writes=["xnT"])

            if c == 0:
                dump("xnT", xnT[:, :, :], [128, 8, 512], ["xnT"])
            b = nb()
            for kc in range(8):
                R.op("pe", lambda e, kc=kc, b=b: e.matmul(bank(b)[0:16, :], lhsT=wfl[:, kc, :], rhs=xnT[:, kc, :], start=(kc == 0), stop=(kc == 7)),
                     reads=["wfl", "xnT"], writes=pres(b))
            R.op("act", lambda e, b=b: e.activation(out=fe[:, :], in_=bank(b)[0:16, :], func=AF.Exp, scale=-1.0, bias=nbf[:, 0:1]),
                 reads=pres(b) + ["nbf"], writes=["junk"])
            R.op("act", lambda e: e.activation(out=fsp[:, :], in_=fe[:, :], func=AF.Ln, bias=onec[0:16, 0:1]), reads=["junk", "onec"], writes=["junk"])
            cn = cneg[c % 2]
            cp = cneg[(c + 1) % 2]
            R.op("dve", lambda e, cn=cn, cp=cp: e.tensor_tensor_scan(out=cn[:, :], data0=onesf[0:16, :], data1=fsp[:, :],
                                                                     initial=cp[:, 511:512], op0=ALU.mult, op1=ALU.add),
                 reads=["junk", "onesf", ("cneg", (c + 1) % 2)], writes=[("cneg", c % 2)])
            R.op("dve", lambda e, cn=cn: e.tensor_copy(out=kp[:, 0, :], in_=cn[:, :]), reads=[("cneg", c % 2)], writes=[("kp", 0)])
            R.op("dve", lambda e, cn=cn: e.tensor_tensor(out=cres[:, :], in0=cn[:, :], in1=kp[:, 0, :], op=ALU.subtract),
                 reads=[("cneg", c % 2), ("kp", 0)], writes=[("tmpm", 0)])
            R.op("dve", lambda e: e.tensor_copy(out=kp[:, 1, :], in_=cres[:, :]), reads=[("tmpm", 0)], writes=[("kp", 1)])
            R.op("dve", lambda e: e.tensor_tensor(out=cres[:, :], in0=cres[:, :], in1=kp[:, 1, :], op=ALU.subtract),
                 reads=[("tmpm", 0), ("kp", 1)], writes=[("tmpm", 0)])
            R.op("dve", lambda e: e.tensor_copy(out=kp[:, 2, :], in_=cres[:, :]), reads=[("tmpm", 0)], writes=[("kp", 2)])
            R.op("dve", lambda e: e.tensor_scalar(qp[:, :, :], kp[:, :, :], -1.0, None, ALU.mult),
                 reads=[("kp", 0), ("kp", 1), ("kp", 2)], writes=["qp"])
            R.dma("pool", kscr.ap()[:, 67:70, t0:t0 + TC], kp[:, :, :], reads=[("kp", 0), ("kp", 1), ("kp", 2)], writes=[("kscr", c, 2)])
            for j3 in range(3):
                R.dma("pool", Qaug[64 + j3:65 + j3, :, :], qp[:, j3, :], reads=["qp", "Qones"], writes=[("Qc", j3)])
            for j in range(2):
                wv_, wr_ = wget("q%d" % j)
                for f4 in range(4):
                    fc = j * 4 + f4
                    b = nb()
                    proj_fm(b, wv_, wr_, f4 * 128, 128, xn_rhs, ["xnT"])
                    R.op("act", lambda e, fc=fc, b=b: e.activation(out=mqT[:, fc, :], in_=bank(b), func=AF.Copy, scale=0.125),
                         reads=pres(b), writes=["mqT"])
            Qv = Qaug[0:64, :, :].rearrange("p (a two) t -> p two a t", two=2)
            for par in range(2):
                R.dma("pool", Qv[:, par], mqT[par * 64:(par + 1) * 64, :, :], reads=["mqT"], writes=[("Qd", h) for h in range(par, 16, 2)])
            for j in range(2):
                wv_, wr_ = wget("k%d" % j)
                for f4 in range(4):
                    fc = j * 4 + f4
                    b = nb()
                    proj_fm(b, wv_, wr_, f4 * 128, 128, xn_rhs, ["xnT"])
                    R.op("dve", lambda e, fc=fc, b=b: e.tensor_copy(out=kst[:, fc, :], in_=bank(b)), reads=pres(b), writes=["yT"])
            kv = kscr.ap().rearrange("(a two) r t -> two r a t", two=2)
            for par in range(2):
                R.dma("pool", kv[par][0:64, :, t0:t0 + TC], kst[par * 64:(par + 1) * 64, :, :], reads=["yT"], writes=[("kscr", c, par)])
            for j in range(2):
                wv_, wr_ = wget("v%d" % j)
                for tb in range(4):
                    b = nb()
                    for kc in range(8):
                        R.op("pe", lambda e, kc=kc, tb=tb, b=b, wv_=wv_: e.matmul(bank(b), lhsT=xnT[:, kc, tb * 128:(tb + 1) * 128], rhs=wv_[:, kc, :],
                                                                                start=(kc == 0), stop=(kc == 7)),
                             reads=[wr_, "xnT"], writes=pres(b))
                    R.op("dve", lambda e, j=j, tb=tb, b=b: e.tensor_copy(out=vst[:, j * 8:(j + 1) * 8, tb, 0:64],
                                                                        in_=bank(b).rearrange("p (h d) -> p h d", d=64)),
                         reads=pres(b), writes=["vst"])
            R.dma("pool", vscr.ap().rearrange("h p k e -> p h k e")[:, :, c * 4:(c + 1) * 4, :], vst[:, :, :, :],
                  reads=["vst", "vst1", "vst0"], writes=[("vscr", c)])
            for j in range(2):
                wv_, wr_ = wget("fg%d" % j)
                for f4 in range(4):
                    fc = j * 4 + f4
                    b = nb()
                    proj_fm(b, wv_, wr_, f4 * 128, 128, xn_rhs, ["xnT"])
                    tg = gt[fc % 2]
                    R.op("act", lambda e, tg=tg, b=b: e.activation(out=tg[:, :], in_=bank(b), func=AF.Tanh, scale=0.5),
                         reads=pres(b), writes=[("gt", fc % 2)])
                    R.op("dve", lambda e, tg=tg, fc=fc, b=b: e.scalar_tensor_tensor(out=mqT[:, fc, :], in0=tg[:, :], scalar=1.0, in1=bank(b),
                                                                                 op0=ALU.add, op1=ALU.mult),
                         reads=pres(b) + [("gt", fc % 2)], writes=["mqT"])
            Sv = sgT[0:64, :, :].rearrange("p (a two) t -> p two a t", two=2)
            for par in range(2):
                R.dma("pool", Sv[:, par], mqT[par * 64:(par + 1) * 64, :, :], reads=["mqT"], writes=[("sgT", h) for h in range(par, 16, 2)])

            if c == 0:
                dump("Qaug", Qaug[:, :, :], [70, 16, 512], [("Qd", h) for h in range(16)] + [("Qc", j) for j in range(3)] + ["Qones"])
                dump("sgT", sgT[:, :, :], [64, 16, 512], [("sgT", h) for h in range(16)])
                dump("cneg", cneg[0][:, :], [16, 512], [("cneg", 0)])
                dump("vst", vst[:, :, :, :], [128, 16, 4, 66], ["vst", "vst0", "vst1"])
                dump("kst", kst[:, :, :], [128, 8, 512], ["yT"])
            def bg_task(c=c):
                for r_ in T_LOW:
                    R.alias(r_, ["xtm"])
                for i_ in range(2):
                    R.alias(("lm", i_), [("ot", 0)])
                    R.alias(("hh", i_), [("ot", 1)])
                    R.alias(("sga", i_), ["xr"])
                for j in range(2):
                    wv_, wr_ = wget("mq%d" % j)
                    for f4 in range(4):
                        fc = j * 4 + f4
                        b = nb()
                        yield from proj_fm_g(b, wv_, wr_, f4 * 128, 128, xn_rhs, ["xnT"])
                        yield
                        R.op("act", lambda e, fc=fc, b=b: e.activation(out=mqT[:, fc, :], in_=bank(b), func=AF.Copy, scale=1.0 / 16.0),
                             reads=pres(b), writes=["mqT"])
                        yield
                def mem_head(hm, hm2, bk, P, pres_, sid, wv_, wr_):
                    for mb in range(2):
                        for dc in range(2):
                            R.op("pe", lambda e, mb=mb, dc=dc: e.matmul(bank(bk), lhsT=mkT[:, hm * 2 + dc, mb * 128:(mb + 1) * 128],
                                                                        rhs=mqT[:, hm * 2 + dc, :], start=(dc == 0), stop=(dc == 1)),
                                 reads=["mkT", "mqT"], writes=pres(bk))
                        yield
                        R.op("act", lambda e, mb=mb: e.activation(out=P[:, mb, :], in_=bank(bk), func=AF.Exp), reads=pres(bk), writes=pres_)
                        yield
                    for mb in range(2):
                        R.op("pe", lambda e, mb=mb: e.matmul(bank(bk), lhsT=onesb, rhs=P[:, mb, :], start=(mb == 0), stop=(mb == 1)),
                             reads=["onesb"] + pres_, writes=pres(bk))
                    yield
                    rc = recm[sid]
                    R.op("dve", lambda e: e.reciprocal(out=rc[:, :], in_=bank(bk)), reads=pres(bk), writes=[("tmpm", sid)])
                    yield
                    Sg = sga[sid]
                    for dc in range(2):
                        fc = hm * 2 + dc
                        yield from proj_fm_g(bk, wv_, wr_, (hm2 * 2 + dc) * 128, 128, xn_rhs, ["xnT"])
                        yield
                        R.op("act", lambda e: e.activation(out=Sg[:, :], in_=bank(bk), func=AF.Tanh, scale=0.5), reads=pres(bk), writes=[("sga", sid)])
                        yield
                        R.op("dve", lambda e: e.scalar_tensor_tensor(out=Sg[:, :], in0=Sg[:, :], scalar=1.0, in1=bank(bk), op0=ALU.add, op1=ALU.mult),
                             reads=[("sga", sid)] + pres(bk), writes=[("sga", sid)])
                        R.op("dve", lambda e: e.scalar_tensor_tensor(out=Sg[:, :], in0=Sg[:, :], scalar=0.5, in1=rc[:, :], op0=ALU.mult, op1=ALU.mult),
                             reads=[("sga", sid), ("tmpm", sid)], writes=[("sga", sid)])
                        yield
                        for mb in range(2):
                            R.op("pe", lambda e, mb=mb, fc=fc: e.matmul(bank(bk), lhsT=mv[:, mb, fc * 128:(fc + 1) * 128], rhs=P[:, mb, :],
                                                                        start=(mb == 0), stop=(mb == 1)),
                                 reads=["mv"] + pres_, writes=pres(bk))
                        yield
                        R.op("dve", lambda e, fc=fc: e.tensor_tensor(out=yT[:, fc, :], in0=Sg[:, :], in1=bank(bk), op=ALU.mult),
                             reads=[("sga", sid)] + pres(bk), writes=["yT"])
                        yield

                for j in range(2):
                    wv_, wr_ = wget("mg%d" % j)
                    yield from rr2(mem_head(2 * j, 0, 6, xcbt, [("xcb", 0), ("xcb", 1)], 0, wv_, wr_),
                                   mem_head(2 * j + 1, 1, 7, Pm1, ["Pm1"], 1, wv_, wr_))
                if c == 0:
                    dump("merged0", merged[:, :, :], [128, 8, 512], [("merged", o) for o in range(8)])
                    dump("yTm", yT[:, :, :], [128, 8, 512], ["yT"])
                yield from branch_merge(2, c, ["b2_0", "b2_1"], ["mr2_0", "mr2_1"], lambda kc: yT[:, kc, :], ["yT"], 8, 128, True, False)
                def lru_fc(fc, bk, wax, rax, wag, rag, f4):
                    i2 = fc % 2
                    b = bk
                    yield from proj_fm_g(b, wax, rax, f4 * 128, 128, xn_rhs, ["xnT"])
                    A = axe[i2]
                    R.op("pool", lambda e, A=A, fc=fc: e.tensor_copy(out=A[:, 0:3], in_=axcar[:, fc, :]), reads=["axcar"], writes=[("axe", i2)])
                    yield
                    R.op("act", lambda e, A=A, b=b: e.activation(out=A[:, 3:515], in_=bank(b), func=AF.Copy), reads=pres(b), writes=[("axe", i2)])
                    R.op("pool", lambda e, A=A, fc=fc: e.tensor_copy(out=axcar[:, fc, :], in_=A[:, 512:515]), reads=[("axe", i2)], writes=["axcar"])
                    yield
                    Sg = sga[i2]
                    bg = bk
                    yield from proj_fm_g(bg, wag, rag, f4 * 128, 128, xn_rhs, ["xnT"])
                    yield
                    R.op("act", lambda e, bg=bg, Sg=Sg: e.activation(out=Sg[:, :], in_=bank(bg), func=AF.Tanh, scale=0.5), reads=pres(bg), writes=[("sga", i2)])
                    yield
                    R.op("dve", lambda e, bg=bg, Sg=Sg: e.scalar_tensor_tensor(out=Sg[:, :], in0=Sg[:, :], scalar=1.0, in1=bank(bg), op0=ALU.add, op1=ALU.mult),
                         reads=[("sga", i2)] + pres(bg), writes=[("sga", i2)])
                    yield
                    X = xc[i2]
                    R.op("dve", lambda e, A=A, X=X, fc=fc: e.tensor_scalar(X[:, :], A[:, 3:515], small[:, 8 + fc * 4 + 3:8 + fc * 4 + 4],
                                                                          small[:, 40 + fc:41 + fc], ALU.mult, ALU.add),
                         reads=[("axe", i2), "small"], writes=[("xc", i2)])
                    yield
                    for k in (2, 1, 0):
                        R.op("dve", lambda e, A=A, X=X, fc=fc, k=k: e.scalar_tensor_tensor(out=X[:, :], in0=A[:, k:k + 512],
                                                                                          scalar=small[:, 8 + fc * 4 + k:8 + fc * 4 + k + 1],
                                                                                          in1=X[:, :], op0=ALU.mult, op1=ALU.add),
                             reads=[("axe", i2), ("xc", i2), "small"], writes=[("xc", i2)])
                        yield
                    XB = xcb[i2]
                    R.op("pool", lambda e, X=X, XB=XB: e.tensor_copy(out=XB, in_=X[:, :]), reads=[("xc", i2)], writes=[("xcb", i2)])
                    yield
                    br = bk
                    R.op("pe", lambda e, br=br, XB=XB, fc=fc: e.matmul(bank(br), lhsT=lruW[:, 0, fc, :], rhs=XB, start=True, stop=True),
                         reads=["lruW", ("xcb", i2)], writes=pres(br))
                    yield
                    Rr, Ii, Aa, Mm, Hh, Sg = lr[i2], li[i2], la[i2], lm[i2], hh[i2], sga[i2]
                    R.op("act", lambda e, br=br, Rr=Rr, fc=fc: e.activation(out=Rr[:, :], in_=bank(br), func=AF.Tanh, scale=0.5, bias=small[:, 48 + fc:49 + fc]),
                         reads=pres(br) + ["small"], writes=[("lr", i2)])
                    yield
                    bi = bk
                    R.op("pe", lambda e, bi=bi, XB=XB, fc=fc: e.matmul(bank(bi), lhsT=lruW[:, 1, fc, :], rhs=XB, start=True, stop=True),
                         reads=["lruW", ("xcb", i2)], writes=pres(bi))
                    yield
                    R.op("act", lambda e, bi=bi, Ii=Ii, fc=fc: e.activation(out=Ii[:, :], in_=bank(bi), func=AF.Tanh, scale=0.5, bias=small[:, 56 + fc:57 + fc]),
                         reads=pres(bi) + ["small"], writes=[("li", i2)])
                    yield
                    R.op("act", lambda e, Rr=Rr, Aa=Aa, fc=fc: e.activation(out=Aa[:, :], in_=Rr[:, :], func=AF.Exp, scale=spc[:, fc:fc + 1], bias=spc[:, fc:fc + 1]),
                         reads=[("lr", i2), "spc"], writes=[("la", i2)])
                    yield
                    R.op("act", lambda e, Rr=Rr, Mm=Mm, fc=fc: e.activation(out=Mm[:, :], in_=Rr[:, :], func=AF.Exp, scale=spc[:, 8 + fc:9 + fc], bias=spc[:, 8 + fc:9 + fc]),
                         reads=[("lr", i2), "spc2"], writes=[("lm", i2)])
                    yield
                    R.op("act", lambda e, Mm=Mm: e.activation(out=Mm[:, :], in_=Mm[:, :], func=AF.Sqrt, scale=-1.0, bias=onec[:, 0:1]),
                         reads=[("lm", i2), "onec"], writes=[("lm", i2)])
                    yield
                    R.op("dve", lambda e, Ii=Ii, X=X: e.scalar_tensor_tensor(out=Ii[:, :], in0=Ii[:, :], scalar=1.0, in1=X[:, :], op0=ALU.add, op1=ALU.mult),
                         reads=[("li", i2), ("xc", i2)], writes=[("li", i2)])
                    R.op("dve", lambda e, Ii=Ii, Mm=Mm: e.scalar_tensor_tensor(out=Ii[:, :], in0=Ii[:, :], scalar=0.5, in1=Mm[:, :], op0=ALU.mult, op1=ALU.mult),
                         reads=[("li", i2), ("lm", i2)], writes=[("li", i2)])
                    yield
                    hp = hlast[(c + 1) % 2]
                    hc = hlast[c % 2]
                    R.op("dve", lambda e, Hh=Hh, Aa=Aa, Ii=Ii, hp=hp, fc=fc: e.tensor_tensor_scan(out=Hh[:, :], data0=Aa[:, :], data1=Ii[:, :],
                                                                                                 initial=hp[:, fc:fc + 1], op0=ALU.mult, op1=ALU.add),
                         reads=[("la", i2), ("li", i2), ("hlast", (c + 1) % 2)], writes=[("hh", i2)])
                    R.op("pool", lambda e, Hh=Hh, hc=hc, fc=fc: e.tensor_copy(out=hc[:, fc:fc + 1], in_=Hh[:, 511:512]),
                         reads=[("hh", i2)], writes=[("hlast", c % 2)])
                    yield
                    R.op("dve", lambda e, Hh=Hh, Sg=Sg, fc=fc: e.scalar_tensor_tensor(out=yT[:, fc, :], in0=Sg[:, :], scalar=0.5, in1=Hh[:, :], op0=ALU.mult, op1=ALU.mult),
                         reads=[("hh", i2), ("sga", i2)], writes=["yT"])
                    yield

                for j in range(2):
                    wax, rax = wget("ax%d" % j)
                    wag, rag = wget("ag%d" % j)
                    for pair in range(2):
                        fa, fb = j * 4 + pair * 2, j * 4 + pair * 2 + 1
                        yield from rr2(lru_fc(fa, 6, wax, rax, wag, rag, fa % 4), lru_fc(fb, 7, wax, rax, wag, rag, fb % 4))
                if c == 0:
                    dump("yTa", yT[:, :, :], [128, 8, 512], ["yT"])
                yield from branch_merge(0, c, ["b0_0", "b0_1"], ["mr0_0", "mr0_1"], lambda kc: yT[:, kc, :], ["yT"], 8, 128, False, False)

                if c + 1 < nch:
                    t1_ = (c + 1) * TC
                    R.alias("xtm", T_LOW)
                    R.alias(("ot", 0), [("lm", 0), ("lm", 1)])
                    R.dma("sp", xtm[:, :, :], x_t.ap()[t1_:t1_ + TC, :].rearrange("(t p) d -> p t d", p=128), writes=["xtm"])
                    yield
                    for tb in range(4):
                        R.op("act", lambda e, tb=tb: e.activation(out=ot[0], in_=xtm[:, tb, :], func=AF.Square, accum_out=ssq[:, tb:tb + 1]),
                             reads=["xtm"], writes=[("ot", 0), ("ssq", tb)])
                        yield
                    R.op("act", lambda e: e.activation(out=stdt[:, 0:4], in_=ssq[:, 0:4], func=AF.Sqrt, scale=1.0 / D, bias=epsc[:, 0:1]),
                         reads=[("ssq", t) for t in range(4)] + ["epsc"], writes=[("std", 0)])
                    yield
                    R.op("dve", lambda e: e.reciprocal(out=rstd[:, 0:4], in_=stdt[:, 0:4]), reads=[("std", 0)], writes=[("rstd", 0)])
                    yield
                    for tb in range(4):
                        R.op("dve", lambda e, tb=tb: e.tensor_scalar(xtm[:, tb, :], xtm[:, tb, :], rstd[:, tb:tb + 1], None, ALU.mult),
                             reads=["xtm", ("rstd", 0)], writes=["xtm"])
                        yield

            nkb = 4 * (c + 1)
            groups = []
            segs = []
            for h in range(16):
                nseg = (nkb * 128 + SEG - 1) // SEG
                for sg in range(nseg):
                    n = kvstate["n"]
                    kvstate["n"] += 1
                    slot = n % NKB
                    k0 = sg * (SEG // 128)
                    k1 = min(nkb, k0 + SEG // 128)
                    c0_, c1_ = (k0 * 128) // TC, min(c + 1, (k1 * 128 + TC - 1) // TC)
                    kres = [("kscr", cc, p) for cc in range(c0_, c1_) for p in range(3)] + [("kones", h)]
                    vres = [("vscr", cc) for cc in range(c0_, c1_)]
                    segs.append((h, k0, k1, slot, kres, vres))
                    si = len(segs) - 1
                    for kb in range(k0, k1):
                        groups.append([h, kb, slot, k0, si, kb + 1 >= k1])

            def issue_load(si):
                if si >= len(segs):
                    return
                h, k0, k1, slot, kres, vres = segs[si]
                ntok = (k1 - k0) * 128
                R.dma("pool", kbuf[slot][:, 0:ntok], kscr.ap()[h, :, k0 * 128:k0 * 128 + ntok], reads=kres, writes=[("kbuf", slot)])
                R.dma("pool", vbuf[slot][:, 0:k1 - k0, :], vscr.ap()[h, :, k0:k1, :], reads=vres, writes=[("vbuf", slot)])
                if c == 0 and si == 0:
                    dump("kb0", kbuf[slot][:, 0:512], [70, 512], [("kbuf", slot)])
                    dump("vb0", vbuf[slot][:, 0:4, :], [128, 4, 66], [("vbuf", slot)])

            for si in range(NKB):
                issue_load(si)
            sbanks = [0, 1, 2]
            BC = 3
            for i_ in range(4):
                R.alias(("jk", i_), ["junk"])

            def emit_S(gi):
                h, kb, slot, k0 = groups[gi][:4]
                bs = sbanks[gi % 3]
                jd = kb - 4 * c
                lo = (kb - k0) * 128
                rd = [("kbuf", slot), ("Qd", h), ("Qc", 0), ("Qc", 1), ("Qc", 2), "Qones"]
                P = PT[gi % NPT]
                if jd < 0:
                    R.op("pe", lambda e: e.matmul(bank(bs), lhsT=kbuf[slot][:, lo:lo + 128], rhs=Qaug[:, h, :], start=True, stop=True),
                         reads=rd, writes=pres(bs))
                    R.op("act", lambda e: e.activation(out=P[:, :], in_=bank(bs), func=AF.Exp), reads=pres(bs), writes=[("PT", gi % NPT)])
                else:
                    q0 = jd * 128
                    R.op("pe", lambda e: e.matmul(bank(bs)[:, q0:q0 + 128], lhsT=kbuf[slot][:, lo:lo + 128], rhs=Qaug[:, h, q0:q0 + 128],
                                                  start=True, stop=False),
                         reads=rd, writes=pres(bs))
                    R.op("pe", lambda e: e.matmul(bank(bs)[:, q0:q0 + 128], lhsT=identb[:, :], rhs=maskb[:, :], start=False, stop=True),
                         reads=["identb", "maskb"], writes=pres(bs))
                    if q0 + 128 < 512:
                        R.op("pe", lambda e: e.matmul(bank(bs)[:, q0 + 128:512], lhsT=kbuf[slot][:, lo:lo + 128], rhs=Qaug[:, h, q0 + 128:512],
                                                      start=True, stop=True),
                             reads=rd, writes=pres(bs))
                    R.op("act", lambda e: e.activation(out=P[:, q0:512], in_=bank(bs)[:, q0:512], func=AF.Exp),
                         reads=pres(bs), writes=[("PT", gi % NPT)])

            def emit_PV(gi):
                h, kb, slot, k0 = groups[gi][:4]
                P = PT[gi % NPT]
                bo = 4 + (h % 2)
                q0 = max(kb - 4 * c, 0) * 128
                R.op("pe", lambda e: e.matmul(bank(bo)[0:65, q0:512], lhsT=vbuf[slot][:, kb - k0, 0:65], rhs=P[:, q0:512],
                                              start=(kb == 0), stop=(kb == nkb - 1), skip_group_check=True),
                     reads=[("vbuf", slot), ("PT", gi % NPT)], writes=pres(bo))
                if kb == nkb - 1:
                    rr_ = recrow[h % 2]
                    R.op("dve", lambda e: e.reciprocal(out=rr_[64:65, :], in_=bank(bo)[64:65, :]), reads=pres(bo), writes=[("jk", 2 + h % 2)])
                    return h
                return None

            def emit_norm(h):
                bo = 4 + (h % 2)
                rr_ = recrow[h % 2]
                tn = tmpn[h % 2]
                R.op("pe", lambda e: e.matmul(bank(BC)[0:64, :], lhsT=onec[64:65, 0:64], rhs=rr_[64:65, :], start=True, stop=True),
                     reads=["onec", ("jk", 2 + h % 2)], writes=pres(BC))
                R.op("dve", lambda e: e.scalar_tensor_tensor(out=tn, in0=sgT[0:64, h, :], scalar=0.5, in1=bank(BC)[0:64, :], op0=ALU.mult, op1=ALU.mult),
                     reads=[("sgT", h)] + pres(BC), writes=[("jk", h % 2)])
                R.op("dve", lambda e: e.tensor_tensor(out=ybT[0:64, h, :], in0=tn, in1=bank(bo)[0:64, :], op=ALU.mult),
                     reads=[("jk", h % 2)] + pres(bo), writes=[("sgT", h)])

            pend = []
            ng = len(groups)
            bgs = {"it": bg_task(), "done": False}

            def bg_step(n):
                for _ in range(n):
                    if bgs["done"]:
                        return
                    try:
                        next(bgs["it"])
                        bgs["n"] = bgs.get("n", 0) + 1
                    except StopIteration:
                        bgs["done"] = True

            NBG = 520
            rate = NBG / float(ng) * 1.15
            acc = {"v": 0.0}
            state["bg"] = True
            state["bb"] = 0
            bg_step(8)
            emit_S(0)
            if ng > 1:
                emit_S(1)
            for gi in range(ng):
                if gi + 2 < ng:
                    emit_S(gi + 2)
                fin = emit_PV(gi)
                if groups[gi][5]:
                    issue_load(groups[gi][4] + NKB)
                if pend:
                    emit_norm(pend.pop(0))
                if fin is not None:
                    pend.append(fin)
                acc["v"] += rate
                nstep = int(acc["v"])
                acc["v"] -= nstep
                bg_step(nstep)
            while pend:
                emit_norm(pend.pop(0))
            state["bg"] = False
            state["b"] = 0
            bg_step(10 ** 6)
            if c == 0 and NBG_PRINT:
                print("NBG stages:", bgs.get("n"))
            R.alias("junk", [("jk", i_) for i_ in range(4)])

            if c == 0:
                dump("merged2", merged[:, :, :], [128, 8, 512], [("merged", o) for o in range(8)])
                dump("ybT", ybT[:, :, :], [64, 16, 512], [("sgT", h) for h in range(16)])
            Yv = ybT[0:64, :, :].rearrange("p (a two) t -> p two a t", two=2)
            for par in range(2):
                R.dma("pool", mqT[par * 64:(par + 1) * 64, :, :], Yv[:, par], reads=[("sgT", h) for h in range(par, 16, 2)], writes=["mqT"])
            for _ in branch_merge(1, c, ["b1_0", "b1_1"], ["mr1_0", "mr1_1"], lambda kc: mqT[:, kc, :], ["mqT"], 8, 128, False, True):
                pass

            if c == 0:
                dump("mergedT", mergedT[:, :, :], [128, 8, 512], ["yT"])
            wo = [wget("wo0"), wget("wo1")]
            R.alias(("ot", 0), [("lm", 0), ("lm", 1)])
            R.alias(("ot", 1), [("hh", 0), ("hh", 1)])
            R.alias("xr", [("sga", 0), ("sga", 1)])
            for tb in range(4):
                i2 = tb % 2
                R.dma("sp", xr[i2][:, :], x_t.ap()[t0 + tb * 128:t0 + (tb + 1) * 128, :], writes=["xr"])
                bs = nb(2)
                for half in range(2):
                    wv_, wr_ = wo[half]
                    for kc in range(8):
                        R.op("pe", lambda e, kc=kc, tb=tb, half=half, bs=bs, wv_=wv_: e.matmul(bank(bs + half), lhsT=mergedT[:, kc, tb * 128:(tb + 1) * 128],
                                                                                             rhs=wv_[:, kc, :], start=(kc == 0), stop=(kc == 7)),
                             reads=[wr_] + ["yT"], writes=pres(bs + half))
                R.op("act", lambda e, bs=bs, tb=tb: e.activation(out=junk[:, :], in_=bank(bs, 2), func=AF.Square, accum_out=ssq2[:, tb:tb + 1]),
                     reads=pres(bs, 2), writes=["junk", ("ssq2", tb)])
                R.op("act", lambda e, tb=tb: e.activation(out=std2[:, tb:tb + 1], in_=ssq2[:, tb:tb + 1], func=AF.Sqrt, scale=1.0 / D, bias=eps4[:, 0:1]),
                     reads=[("ssq2", tb), "eps4"], writes=[("std2", tb)])
                R.op("dve", lambda e, tb=tb: e.reciprocal(out=rstd2[:, tb:tb + 1], in_=std2[:, tb:tb + 1]), reads=[("std2", tb)], writes=[("rstd2", tb)])
                O = ot[i2]
                R.op("dve", lambda e, O=O, bs=bs, tb=tb: e.scalar_tensor_tensor(out=O[:, :], in0=bank(bs, 2), scalar=rstd2[:, tb:tb + 1], in1=gpost[:, :],
                                                                              op0=ALU.mult, op1=ALU.mult),
                     reads=pres(bs, 2) + [("rstd2", tb), "gpost"], writes=[("ot", i2)])
                R.op("pool", lambda e, O=O, i2=i2: e.tensor_tensor(out=O[:, :], in0=O[:, :], in1=xr[i2][:, :], op=ALU.add),
                     reads=[("ot", i2), "xr"], writes=[("ot", i2)])
                R.dma("pool", y_t.ap()[t0 + tb * 128:t0 + (tb + 1) * 128, :], O[:, :], reads=[("ot", i2)], writes=[("y", c, tb)])

        assert wstate["pos"] == len(wseq), (wstate["pos"], len(wseq))
        block = st.enter_context(nc.Block())
        R.emit(nc, block, st)
    nc._dbg_names = list(dbg_out.keys())
    return nc


_CACHE = {}


def _host_consts(inp):
    f32 = np.float32
    small = np.zeros((128, 80), f32)

    def col(v):
        return np.ascontiguousarray(np.asarray(v, f32).reshape(8, 128).T)

    small[:, 0:8] = col(inp["g_pre"])
    cw = np.asarray(inp["conv_w"], f32)
    for k in range(4):
        small[:, 8 + k:40:4] = col(cw[k])
    small[:, 40:48] = col(inp["conv_b"])
    small[:, 48:56] = col(np.asarray(inp["b_lru_r"], f32).reshape(-1))
    small[:, 56:64] = col(np.asarray(inp["b_lru_i"], f32).reshape(-1))
    small[:, 64:72] = col(inp["lru_lambda"])
    small[:, 72:80] = col(inp["g_mem"])
    bm = np.zeros((128, 24), f32)
    b_merge = np.asarray(inp["b_merge"], f32)
    for n in range(3):
        bm[:, n * 8:(n + 1) * 8] = col(b_merge[n])
    lruw = np.zeros((2, 128, 8, 128), f32)
    for g, w in enumerate((np.asarray(inp["w_lru_r"], f32), np.asarray(inp["w_lru_i"], f32))):
        for blk in range(16):
            fc, half = blk // 2, blk % 2
            lruw[g, half * 64:(half + 1) * 64, fc, half * 64:(half + 1) * 64] = w[blk]
    ident = np.eye(128, dtype=f32)
    kk = np.arange(128)[:, None]
    qq = np.arange(128)[None, :]
    maskT = np.where(kk > qq, f32(NEG), f32(0.0)).astype(f32)
    ones = np.ones((128, 512), f32)
    return {
        "small": small, "bmerge": bm, "lruw": lruw, "ident": ident, "maskT": maskT, "ones": ones,
        "b_forget": np.asarray(inp["b_forget"], f32).reshape(16, 1),
        "g_post": np.ascontiguousarray(np.broadcast_to(np.asarray(inp["g_post"], f32).reshape(1, D), (128, D))),
        "w_in": np.ascontiguousarray(inp["w_in"], dtype=f32),
        "w_mem_k": np.ascontiguousarray(inp["w_mem_k"], dtype=f32),
        "w_mem_v": np.ascontiguousarray(inp["w_mem_v"], dtype=f32),
        "w_branch": np.ascontiguousarray(inp["w_branch"], dtype=f32),
        "w_out": np.ascontiguousarray(inp["w_out"], dtype=f32),
    }


def kernel(nch=NCH_FULL, cores=8, _dbgres=None, **inp):
    if nch not in _CACHE:
        _CACHE[nch] = build(nch)
    nc = _CACHE[nch]
    shared = _host_consts(inp)
    x = np.asarray(inp["x"], np.float32)
    mem = np.asarray(inp["mem"], np.float32)
    in_maps = []
    for b in range(cores):
        m = dict(shared)
        m["x"] = np.ascontiguousarray(x[b])
        m["mem"] = np.ascontiguousarray(mem[b])
        in_maps.append(m)
    res = run_bass_kernel_spmd(nc, in_maps, core_ids=list(range(cores)))
    out = np.stack([np.asarray(r["y"], np.float32) for r in res.results], axis=0)
    if _dbgres is not None:
        for n in getattr(nc, "_dbg_names", []):
            _dbgres[n] = np.asarray(res.results[0]["dbg_" + n])
    return out
```
